# Optimizing a Trainium2 kernel written in Bass

```python
import jax, jax.numpy as jnp
from jax import lax
import numpy as np

D_MODEL = 1024
BATCH = 8
SEQ = 4096
DEPTH = 1

CHUNK = 64
Q_BLOCK = 128
SB_HEADS = 8
SB_HEAD_DIM = 64
SB_WIDTH = SB_HEADS * SB_HEAD_DIM
GLA_HEADS = 4
GLA_KEY_WIDTH = D_MODEL // 2
GLA_VALUE_WIDTH = D_MODEL
GLA_DK = GLA_KEY_WIDTH // GLA_HEADS
GLA_DV = GLA_VALUE_WIDTH // GLA_HEADS
GLA_GATE_RANK = 16
GLA_GATE_TAU = 16.0
D_FF = 4 * D_MODEL
EPS = 1e-6

SPLIT_SIZES = (SB_WIDTH, SB_WIDTH, SB_WIDTH,
               GLA_KEY_WIDTH, GLA_KEY_WIDTH, GLA_VALUE_WIDTH, GLA_VALUE_WIDTH, GLA_GATE_RANK,
               D_MODEL, D_MODEL)
D_IN = 3 * SB_WIDTH + 2 * GLA_KEY_WIDTH + 2 * GLA_VALUE_WIDTH + GLA_GATE_RANK + 2 * D_MODEL

kernel_name = 'hybrid_stickbreak_gla_sqrelu_block'


def rms_norm(x, g):
    xf = x.astype(jnp.float32)
    y = xf * lax.rsqrt(jnp.mean(xf * xf, axis=-1, keepdims=True) + EPS)
    return (y * g.astype(jnp.float32)).astype(x.dtype)


def split_heads(t, n_heads):
    b, s, _ = t.shape
    return t.reshape(b, s, n_heads, -1).transpose(0, 2, 1, 3)


def merge_heads(t):
    b, h, s, d = t.shape
    return t.transpose(0, 2, 1, 3).reshape(b, s, h * d)


def stick_breaking_attention(q, k, v):
    _, _, s_len, d = q.shape
    scale = d ** -0.5
    outs = []
    for start in range(0, s_len, Q_BLOCK):
        end = start + Q_BLOCK
        kb = k[:, :, :end]
        vb = v[:, :, :end]
        z = jnp.einsum('bhtd,bhsd->bhts', q[:, :, start:end], kb) * scale
        t_idx = start + jnp.arange(Q_BLOCK)[:, None]
        s_idx = jnp.arange(end)[None, :]
        past = s_idx < t_idx
        log_beta = jax.nn.log_sigmoid(z)
        log_fail = jnp.where(past, log_beta - z, 0.0)
        later = lax.cumsum(log_fail, axis=3, reverse=True) - log_fail
        w = jnp.where(past, jnp.exp(log_beta + later), 0.0)
        outs.append(jnp.einsum('bhts,bhsd->bhtd', w, vb))
    return jnp.concatenate(outs, axis=2)


def gla_chunked(q, k, v, log_a):
    b, h, s_len, dk = q.shape
    dv = v.shape[-1]
    nc = s_len // CHUNK

    def to_chunks(t):
        return t.reshape(b, h, nc, CHUNK, t.shape[-1]).transpose(2, 0, 1, 3, 4)

    qc, kc, vc, ac = (to_chunks(t) for t in (q * dk ** -0.5, k, v, log_a))
    bc = jnp.cumsum(ac, axis=3)
    causal = jnp.tril(jnp.ones((CHUNK, CHUNK), dtype=bool))

    def step(state, inp):
        qi, ki, vi, bi = inp
        b_last = bi[:, :, -1:, :]
        o_inter = jnp.einsum('bhtk,bhkv->bhtv', qi * jnp.exp(bi), state)
        diff = bi[:, :, :, None, :] - bi[:, :, None, :, :]
        decay = jnp.exp(jnp.where(causal[:, :, None], diff, -jnp.inf))
        scores = jnp.einsum('bhtk,bhsk,bhtsk->bhts', qi, ki, decay)
        o_intra = jnp.einsum('bhts,bhsv->bhtv', scores, vi)
        new_state = (jnp.exp(b_last[:, :, 0, :, None]) * state
                     + jnp.einsum('bhsk,bhsv->bhkv', ki * jnp.exp(b_last - bi), vi))
        return new_state, o_inter + o_intra

    state0 = jnp.zeros((b, h, dk, dv), jnp.float32)
    _, o = lax.scan(step, state0, (qc, kc, vc, bc))
    return o.transpose(1, 2, 0, 3, 4).reshape(b, h, s_len, dv)


def setup_inputs(seed: int = 0) -> dict:
    key = jax.random.key(seed)
    ks = jax.random.split(key, 16)
    f32 = jnp.float32

    def w(k, shape, fan_in):
        return jax.random.normal(k, shape, f32) * (fan_in ** -0.5)

    def gain(k, shape):
        return 1.0 + 0.02 * jax.random.normal(k, shape, f32)

    return {
        'x': jax.random.normal(ks[0], (BATCH, SEQ, D_MODEL), f32),
        'norm_mix': gain(ks[1], (DEPTH, D_MODEL)),
        'w_in': w(ks[2], (DEPTH, D_MODEL, D_IN), D_MODEL),
        'w_gate_up': w(ks[3], (DEPTH, GLA_GATE_RANK, GLA_KEY_WIDTH), GLA_GATE_RANK),
        'b_gate_up': 0.1 * jax.random.normal(ks[4], (DEPTH, GLA_KEY_WIDTH), f32),
        'gla_norm': gain(ks[5], (DEPTH, GLA_DV)),
        'w_proj_sb': w(ks[6], (DEPTH, SB_WIDTH, D_MODEL), SB_WIDTH),
        'w_proj_gla': w(ks[7], (DEPTH, GLA_VALUE_WIDTH, D_MODEL), GLA_VALUE_WIDTH),
        'w_out': w(ks[8], (DEPTH, D_MODEL, D_MODEL), D_MODEL),
        'norm_mlp': gain(ks[9], (DEPTH, D_MODEL)),
        'w_ff1': w(ks[10], (DEPTH, D_MODEL, D_FF), D_MODEL),
        'w_ff2': w(ks[11], (DEPTH, D_FF, D_MODEL), D_FF),
        'norm_final': gain(ks[12], (D_MODEL,)),
    }


def reference(x, norm_mix, w_in, w_gate_up, b_gate_up, gla_norm, w_proj_sb, w_proj_gla,
              w_out, norm_mlp, w_ff1, w_ff2, norm_final):
    f32 = jnp.float32
    bsz, s_len, _ = x.shape
    split_points = np.cumsum(SPLIT_SIZES)[:-1].tolist()
    for layer in range(DEPTH):
        h = rms_norm(x, norm_mix[layer])
        proj = h @ w_in[layer]
        (sb_q, sb_k, sb_v, g_q, g_k, g_v, g_out, g_low, gate_sb, gate_gla) = jnp.split(
            proj, split_points, axis=-1)

        o_sb = stick_breaking_attention(split_heads(sb_q, SB_HEADS).astype(f32),
                                        split_heads(sb_k, SB_HEADS).astype(f32),
                                        split_heads(sb_v, SB_HEADS).astype(f32))
        o_sb = merge_heads(o_sb).astype(x.dtype)

        log_a = jax.nn.log_sigmoid((g_low @ w_gate_up[layer] + b_gate_up[layer]).astype(f32)) / GLA_GATE_TAU
        o_gla = gla_chunked(split_heads(g_q, GLA_HEADS).astype(f32),
                            split_heads(g_k, GLA_HEADS).astype(f32),
                            split_heads(g_v, GLA_HEADS).astype(f32),
                            split_heads(log_a, GLA_HEADS))
        o_gla = o_gla.transpose(0, 2, 1, 3)
        o_gla = rms_norm(o_gla, gla_norm[layer])
        o_gla = o_gla * jax.nn.silu(g_out.astype(f32).reshape(bsz, s_len, GLA_HEADS, GLA_DV))
        o_gla = o_gla.reshape(bsz, s_len, GLA_VALUE_WIDTH).astype(x.dtype)

        mix = (jax.nn.sigmoid(gate_sb) * (o_sb @ w_proj_sb[layer])
               + jax.nn.sigmoid(gate_gla) * (o_gla @ w_proj_gla[layer]))
        x = x + mix @ w_out[layer]

        h2 = rms_norm(x, norm_mlp[layer])
        x = x + jnp.square(jax.nn.relu(h2 @ w_ff1[layer])) @ w_ff2[layer]
    return rms_norm(x, norm_final)
```

```python
from contextlib import ExitStack

import numpy as np
import concourse.bass as bass
import concourse.mybir as mybir
from concourse.bass_utils import run_bass_kernel_spmd

F32 = mybir.dt.float32
BF16 = mybir.dt.bfloat16
AF = mybir.ActivationFunctionType
ALU = mybir.AluOpType

D = 1024
DIN = 6672
DFF = 4096
EPS = 1e-6
N_CORES = 8
SEQ = 4096
C_SBQ, C_SBK, C_SBV = 0, 512, 1024
C_GQ, C_GK, C_GV, C_GO, C_LOW, C_GA, C_GB = 1536, 2048, 2560, 3584, 4608, 4624, 5648
NDMA = 48
NDMA_SW = 16


class Op:
    __slots__ = ("eng", "fn", "deps", "need_sig", "sigval", "is_dma", "dsem", "dval")


class Prog:
    def __init__(self, nc, es):
        self.nc = nc
        self.engs = {"pe": nc.tensor, "act": nc.scalar, "dve": nc.vector, "pool": nc.gpsimd, "sp": nc.sync}
        self.sems = {e: es.enter_context(nc.semaphore("s_" + e)) for e in ("pe", "act", "dve", "pool")}
        self.cnt = {e: 0 for e in self.sems}
        self.dsems = [es.enter_context(nc.semaphore("d%d" % i)) for i in range(NDMA)]
        self.dcnt = [0] * NDMA
        self.drr = 0
        self.drr_sw = 0
        self.last_w = {}
        self.readers = {}
        self.pending = []
        self.waited = {}
        self.nops = 0

    def op(self, eng, fn, R=(), W=(), dma=False):
        o = Op()
        o.eng, o.fn, o.is_dma = eng, fn, dma
        o.need_sig, o.sigval, o.dsem, o.dval = False, None, None, None
        deps = {}
        for k in R:
            w = self.last_w.get(k)
            if w is not None:
                deps[id(w)] = (w, "raw")
        for k in W:
            w = self.last_w.get(k)
            if w is not None and id(w) not in deps:
                deps[id(w)] = (w, "waw")
            lastr = {}
            for r in self.readers.get(k, ()):
                if r.is_dma:
                    if id(r) not in deps:
                        deps[id(r)] = (r, "war")
                else:
                    lastr[r.eng] = r
            for r in lastr.values():
                if id(r) not in deps:
                    deps[id(r)] = (r, "war")
        for k in W:
            self.last_w[k] = o
            self.readers[k] = []
        for k in R:
            self.readers.setdefault(k, []).append(o)
        o.deps = []
        for d, kind in deps.values():
            if d is o:
                continue
            if (not dma) and (not d.is_dma) and d.eng == eng:
                if eng == "pe":
                    continue
            d.need_sig = True
            o.deps.append(d)
        self.pending.append(o)
        return o

    def _wait(self, eng, key, sem, val):
        if self.waited.get((eng, key), 0) >= val:
            return
        self.waited[(eng, key)] = val
        self.engs[eng].wait_ge(sem, val)

    def flush(self, barrier=False):
        last = {}
        for o in self.pending:
            if not o.is_dma:
                last[o.eng] = o
        for o in last.values():
            o.need_sig = True
        for o in self.pending:
            for d in o.deps:
                if d.is_dma:
                    self._wait(o.eng, ("d", d.dsem), self.dsems[d.dsem], d.dval)
                else:
                    self._wait(o.eng, d.eng, self.sems[d.eng], d.sigval)
            if o.is_dma:
                if o.eng == "pool":
                    i = self.drr_sw % NDMA_SW
                    self.drr_sw += 1
                else:
                    i = NDMA_SW + self.drr % (NDMA - NDMA_SW)
                    self.drr += 1
                if self.dcnt[i] > 0:
                    self._wait(o.eng, ("d", i), self.dsems[i], 16 * self.dcnt[i])
                self.dcnt[i] += 1
                o.dsem, o.dval = i, 16 * self.dcnt[i]
                o.fn().then_inc(self.dsems[i], 16)
            else:
                ins = o.fn()
                if o.need_sig:
                    self.cnt[o.eng] += 1
                    o.sigval = self.cnt[o.eng]
                    ins.then_inc(self.sems[o.eng], 1)
                else:
                    o.sigval = self.cnt[o.eng] + 1
            o.fn = None
            self.nops += 1
        self.pending = []
        if barrier:
            self.barrier()

    def barrier(self, engines=("pe", "act", "dve", "pool", "sp")):
        for e in engines:
            for f in self.sems:
                if f != e and self.cnt[f] > 0:
                    self._wait(e, f, self.sems[f], self.cnt[f])
            for i in range(NDMA):
                if self.dcnt[i] > 0:
                    self._wait(e, ("d", i), self.dsems[i], 16 * self.dcnt[i])

    def mm(self, out, lhsT, rhs, start, stop, R, W):
        nc = self.nc
        return self.op("pe", lambda: nc.tensor.matmul(out, lhsT, rhs, start=start, stop=stop), R, W)

    def tr(self, out, in_, ident, R, W):
        nc = self.nc
        return self.op("pe", lambda: nc.tensor.transpose(out, in_, ident), R, W)

    def act(self, out, in_, func, R, W, bias=None, scale=None, accum=None):
        nc = self.nc
        kw = {}
        if bias is not None:
            kw["bias"] = bias
        if scale is not None:
            kw["scale"] = scale
        if accum is not None:
            kw["accum_out"] = accum
        return self.op("act", lambda: nc.scalar.activation(out=out, in_=in_, func=func, **kw), R, W)

    def tt(self, eng, out, in0, in1, op, R, W):
        e = self.engs[eng]
        return self.op(eng, lambda: e.tensor_tensor(out=out, in0=in0, in1=in1, op=op), R, W)

    def ts(self, eng, out, in0, s1, s2, op0, op1, R, W):
        e = self.engs[eng]
        if op1 is None:
            return self.op(eng, lambda: e.tensor_scalar(out=out, in0=in0, scalar1=s1, scalar2=None, op0=op0), R, W)
        return self.op(eng, lambda: e.tensor_scalar(out=out, in0=in0, scalar1=s1, scalar2=s2, op0=op0, op1=op1), R, W)

    def stt(self, eng, out, in0, scalar, in1, op0, op1, R, W):
        e = self.engs[eng]
        return self.op(eng, lambda: e.scalar_tensor_tensor(out=out, in0=in0, scalar=scalar, in1=in1, op0=op0, op1=op1), R, W)

    def copy(self, eng, out, in_, R, W):
        if eng == "act":
            return self.act(out, in_, AF.Copy, R, W)
        e = self.engs[eng]
        return self.op(eng, lambda: e.tensor_copy(out=out, in_=in_), R, W)

    def recip(self, eng, out, in_, R, W):
        e = self.engs[eng]
        return self.op(eng, lambda: e.reciprocal(out=out, in_=in_), R, W)

    def memset(self, eng, ap, val, W):
        e = self.engs[eng]
        return self.op(eng, lambda: e.memset(ap, val), (), W)

    def dma(self, q, out, in_, R, W):
        e = self.engs[q]
        return self.op(q, lambda: e.dma_start(out=out, in_=in_), R, W, dma=True)


def build_program(S, stop_after=None):
    NT = S // 128
    TGA = min(512, S)
    NGA = S // TGA
    TPA = TGA // 128
    TGC = min(256, S)
    NGC = S // TGC
    TPC = TGC // 128

    nc = bass.Bass("TRN2", target_bir_lowering=False)

    def din(name, shape):
        return nc.dram_tensor(name, list(shape), F32, kind="ExternalInput").ap()

    x = din("x", [S, D])
    w_in = din("w_in", [D, DIN])
    w_up = din("w_up", [17, 512])
    w_pa = din("w_pa", [512, D])
    w_pb = din("w_pb", [D, D])
    w_o = din("w_o", [D, D])
    w_1 = din("w_1", [D, DFF])
    w_2 = din("w_2", [DFF, D])
    g_mix = din("g_mix", [128, D])
    g_mlp = din("g_mlp", [128, D])
    g_fin = din("g_fin", [128, D])
    g_gla = din("g_gla", [128, D])
    c_id = din("c_id", [128, 128])
    c_ui = din("c_ui", [128, 128])
    c_ls = din("c_ls", [128, 128])
    c_md = din("c_md", [128, 512])
    c_tc = din("c_tc", [128, 128])
    c_ta = din("c_ta", [128, 128])
    c_mg = din("c_mg", [128, 512])
    out = nc.dram_tensor("out", [S, D], F32, kind="ExternalOutput").ap()

    sc_f16 = nc.dram_tensor("sc_f16", [1024, S], BF16).ap()
    sc_f32 = nc.dram_tensor("sc_f32", [1024, S], F32).ap()
    sc_low = nc.dram_tensor("sc_low", [16, S], BF16).ap()
    sc_t16 = nc.dram_tensor("sc_t16", [S, 1536], BF16).ap()
    sc_t32 = nc.dram_tensor("sc_t32", [S, 3584], F32).ap()
    sc_x1 = nc.dram_tensor("sc_x1", [S, D], F32).ap()

    with ExitStack() as ges:
        P = Prog(nc, ges)

        def sbg(name, shape, dt):
            return ges.enter_context(nc.sbuf_tensor(name, list(shape), dt))

        ident = sbg("ident", [128, 128], BF16)
        P.dma("pool", ident[:], c_id, (), ["ident"])

        def prenorm(pfx, xs_ap, xkey, gam, gkey, sqj, ss, rs, hbuf, sl, tp, tps, hT_dst, hkey):
            P.act(sqj[sl][:], xs_ap, AF.Square, [xkey], [(pfx + "sqj", sl), (pfx + "ss", sl)], accum=ss[sl][:])
            P.act(rs[sl][:], ss[sl][:], AF.Ln, [(pfx + "ss", sl)], [(pfx + "rs", sl)], bias=EPS, scale=1.0 / D)
            P.act(rs[sl][:], rs[sl][:], AF.Exp, [(pfx + "rs", sl)], [(pfx + "rs", sl)], scale=-0.5)
            P.stt("dve", hbuf[sl][:], xs_ap, rs[sl][:, 0:1], gam[:], ALU.mult, ALU.mult,
                  [xkey, (pfx + "rs", sl), gkey], [(pfx + "hb", sl)])
            for c in range(8):
                P.tr(tp[tps][:, c, :], hbuf[sl][:, c * 128:(c + 1) * 128], ident[:],
                     [(pfx + "hb", sl), "ident"], [(pfx + "tp", tps)])
            P.copy("act", hT_dst, tp[tps][:], [(pfx + "tp", tps)], [hkey])

        with ExitStack() as es:
            def sb(name, shape, dt):
                return es.enter_context(nc.sbuf_tensor(name, list(shape), dt))

            def ps(name, shape, dt):
                return es.enter_context(nc.psum_tensor(name, list(shape), dt))

            Wfm = sb("Wfm", [128, 8, 2064], BF16)
            Wtm = sb("Wtm", [128, 8, 5120], BF16)
            gmx = sb("gmx", [128, D], F32)
            xt = [sb("a_xt%d" % i, [128, D], F32) for i in range(2)]
            sqj = [sb("a_sqj%d" % i, [128, D], BF16) for i in range(2)]
            ss = [sb("a_ss%d" % i, [128, 1], F32) for i in range(2)]
            rs = [sb("a_rs%d" % i, [128, 1], F32) for i in range(2)]
            hb = [sb("a_hb%d" % i, [128, D], BF16) for i in range(2)]
            hT = [sb("a_hT%d" % i, [128, 8, TGA], BF16) for i in range(2)]
            NST = 4
            st16 = [sb("a_st16_%d" % i, [128, 512], BF16) for i in range(NST)]
            st32 = [sb("a_st32_%d" % i, [128, 512], F32) for i in range(NST)]
            tp = [ps("a_tp%d" % i, [128, 8, 128], BF16) for i in range(2)]
            pf = [ps("a_pf%d" % i, [128, 512], F32) for i in range(3)]
            pt = [ps("a_pt%d" % i, [128, 512], F32) for i in range(3)]

            P.dma("sp", gmx[:], g_mix, (), ["gmx"])
            w_in_v = w_in.rearrange("(c p) n -> c p n", p=128)
            fm_ranges = [(C_SBQ, 1024, 0), (C_GQ, 1024, 1024), (C_LOW, 16, 2048)]
            tm_ranges = [(C_SBV, 512, 0), (C_GV, 1024, 512), (C_GK, 512, 1536), (C_GO, 1024, 2048),
                         (C_GA, 2048, 3072)]
            for c in range(8):
                for (s0, n, d0) in fm_ranges:
                    P.dma("pool", Wfm[:, c, d0:d0 + n], w_in_v[c, :, s0:s0 + n], (), [("Wfm", c)])
            for c in range(8):
                for (s0, n, d0) in tm_ranges:
                    P.dma("pool", Wtm[:, c, d0:d0 + n], w_in_v[c, :, s0:s0 + n], (), [("Wtm", c)])

            cntA = {"k16": 0, "k32": 0, "kev": 0, "xi": 0, "kk": 0}

            def prepA(g):
                hs = g % 2
                for i in range(TPA):
                    t0 = g * TGA + i * 128
                    sl = cntA["xi"] % 2
                    cntA["xi"] += 1
                    P.dma("sp", xt[sl][:], x[t0:t0 + 128, :], (), [("xt", sl)])
                    prenorm("a", xt[sl][:], ("xt", sl), gmx, "gmx", sqj, ss, rs, hb, sl, tp, sl,
                            hT[hs][:, :, i * 128:(i + 1) * 128], ("hT", hs))

            def evac(dst, src, eng, R, W, scale=None):
                if scale is None or scale == 1.0:
                    P.copy(eng, dst, src, R, W)
                elif eng == "act":
                    P.act(dst, src, AF.Copy, R, W, scale=scale)
                else:
                    P.ts("dve", dst, src, scale, None, ALU.mult, None, R, W)

            def computeA(g):
                hs = g % 2
                tg0 = g * TGA
                for j in range(17):
                    pb = pf[j % 3]
                    pk = ("pf", j % 3)
                    m = 128 if j < 16 else 16
                    for c in range(8):
                        P.mm(pb[0:m, 0:TGA], Wfm[:, c, j * 128:j * 128 + m], hT[hs][:, c, :], c == 0, c == 7,
                             [("Wfm", c), ("hT", hs)], [pk])
                    ev = "act" if cntA["kev"] % 2 == 0 else "dve"
                    cntA["kev"] += 1
                    if j < 8:
                        k = cntA["k16"] % NST
                        cntA["k16"] += 1
                        stg, key = st16[k], ("st16", k)
                        evac(stg[:, 0:TGA], pb[:, 0:TGA], ev, [pk], [key], 0.125 if j < 4 else None)
                        P.dma("sp", sc_f16[j * 128:(j + 1) * 128, tg0:tg0 + TGA], stg[:, 0:TGA], [key], [("sc_f16", g)])
                    elif j < 16:
                        k = cntA["k32"] % NST
                        cntA["k32"] += 1
                        stg, key = st32[k], ("st32", k)
                        evac(stg[:, 0:TGA], pb[:, 0:TGA], ev, [pk], [key], (128.0 ** -0.5) if j < 12 else None)
                        jj = j - 8
                        P.dma("sp", sc_f32[jj * 128:(jj + 1) * 128, tg0:tg0 + TGA], stg[:, 0:TGA], [key], [("sc_f32", g)])
                    else:
                        k = cntA["k16"] % NST
                        cntA["k16"] += 1
                        stg, key = st16[k], ("st16", k)
                        evac(stg[0:16, 0:TGA], pb[0:16, 0:TGA], ev, [pk], [key])
                        P.dma("sp", sc_low[:, tg0:tg0 + TGA], stg[0:16, 0:TGA], [key], [("sc_low", g)])
                for i in range(TPA):
                    t0 = g * TGA + i * 128
                    for cg in range(10):
                        pb = pt[cntA["kk"] % 3]
                        pk = ("pt", cntA["kk"] % 3)
                        cntA["kk"] += 1
                        for c in range(8):
                            P.mm(pb[:], hT[hs][:, c, i * 128:(i + 1) * 128], Wtm[:, c, cg * 512:(cg + 1) * 512],
                                 c == 0, c == 7, [("Wtm", c), ("hT", hs)], [pk])
                        ev = "act" if cntA["kev"] % 2 == 0 else "dve"
                        cntA["kev"] += 1
                        if cg < 3:
                            k = cntA["k16"] % NST
                            cntA["k16"] += 1
                            stg, key = st16[k], ("st16", k)
                            evac(stg[:], pb[:], ev, [pk], [key])
                            P.dma("sp", sc_t16[t0:t0 + 128, cg * 512:(cg + 1) * 512], stg[:], [key], [("sc_t16", t0 // 128)])
                        else:
                            k = cntA["k32"] % NST
                            cntA["k32"] += 1
                            stg, key = st32[k], ("st32", k)
                            evac(stg[:], pb[:], ev, [pk], [key])
                            P.dma("sp", sc_t32[t0:t0 + 128, (cg - 3) * 512:(cg - 2) * 512], stg[:], [key],
                                  [("sc_t32", t0 // 128)])

            prepA(0)
            for g in range(NGA):
                if g + 1 < NGA:
                    prepA(g + 1)
                computeA(g)
            P.flush(barrier=True)
        if stop_after == "A":
            return nc

        with ExitStack() as es:
            def sb(name, shape, dt):
                return es.enter_context(nc.sbuf_tensor(name, list(shape), dt))

            def ps(name, shape, dt):
                return es.enter_context(nc.psum_tensor(name, list(shape), dt))

            kT = sb("kT", [128, 4, S], BF16)
            vS = sb("vS", [128, NT, 512], BF16)
            wpA = sb("wpA", [64, 8, D], BF16)
            wpB = sb("wpB", [128, 8, D], BF16)
            wout = sb("wout", [128, 8, D], BF16)
            wup = sb("wup", [32, 512], BF16)
            ggl = sb("ggl", [128, D], F32)
            Ui = sb("Ui", [128, 128], BF16)
            Ones = sb("Ones", [128, 128], BF16)
            Rr = [sb("Rr%d" % i, [128, 512], F32) for i in range(2)]
            Rb = [sb("Rb%d" % i, [128, 512], BF16) for i in range(2)]
            Md = sb("Md", [128, 512], BF16)
            Tc = sb("Tc", [128, 128], BF16)
            Ta = sb("Ta", [128, 128], BF16)
            Mg = sb("Mg", [128, 512], F32)
            st32s = sb("st32s", [128, 4, 256], F32)
            stbf = sb("stbf", [128, 4, 256], BF16)
            qTe = sb("qTe", [128, 4, 128], BF16)
            qTo = sb("qTo", [128, 4, 128], BF16)
            gq32 = sb("gq32", [128, 4, 128], F32)
            gk32 = sb("gk32", [128, 4, 128], F32)
            lowA = sb("lowA", [32, 128], BF16)
            tgk = sb("tgk", [128, 512], F32)
            tgo = sb("tgo", [128, D], F32)
            tgA = sb("tgA", [128, D], F32)
            tgB = sb("tgB", [128, D], F32)
            gv16 = sb("gv16", [128, D], BF16)
            xres = sb("xres", [128, D], F32)
            e32 = [sb("e32_%d" % i, [128, 512], F32) for i in range(2)]
            spb = [sb("spb%d" % i, [128, 512], BF16) for i in range(2)]
            g32 = [sb("g32_%d" % i, [128, 512], F32) for i in range(2)]
            wbf = [sb("wbf%d" % i, [128, 512], BF16) for i in range(2)]
            oTb = sb("oTb", [64, 8, 128], BF16)
            eu = sb("eu", [128, 512], F32)
            spg = sb("spg", [128, 512], BF16)
            Eb = sb("Eb", [128, 512], F32)
            Enb = sb("Enb", [128, 512], F32)
            qtil = sb("qtil", [128, 512], BF16)
            ktil = sb("ktil", [128, 512], BF16)
            khat = sb("khat", [128, 512], BF16)
            scb = sb("scb", [128, 512], BF16)
            ssg = sb("ssg", [128, 4], F32)
            rsg = sb("rsg", [128, 4], F32)
            junk = [sb("junkb%d" % i, [128, 256], BF16) for i in range(4)]
            eg = sb("eg", [128, D], F32)
            gs = sb("gs", [128, D], F32)
            ogb = sb("ogb", [128, D], BF16)
            ogT = sb("ogT", [128, 8, 128], BF16)
            x1t = sb("x1t", [128, D], F32)

            Z = [ps("Z%d" % i, [128, 512], F32) for i in range(2)]
            T = [ps("T%d" % i, [128, 512], F32) for i in range(2)]
            O = ps("O", [128, 1024], F32)
            G0 = ps("G0", [128, 512], F32)
            G1 = ps("G1", [128, 8, 128], BF16)

            for cc in range(4):
                P.dma("sp", kT[:, cc, :], sc_f16[512 + cc * 128:512 + (cc + 1) * 128, :],
                      [("sc_f16", g) for g in range(NGA)], [("kT", cc)])
            vsrc = sc_t16.rearrange("(n p) c -> p n c", p=128)
            VB = 8
            for n0 in range(0, NT, VB):
                n1 = min(NT, n0 + VB)
                P.dma("sp", vS[:, n0:n1, :], vsrc[:, n0:n1, 0:512], [("sc_t16", n) for n in range(n0, n1)],
                      [("vS", n) for n in range(n0, n1)])
            for h in range(8):
                P.dma("pool", wpA[:, h, :], w_pa[h * 64:(h + 1) * 64, :], (), ["wpA"])
            for c in range(8):
                P.dma("pool", wpB[:, c, :], w_pb[c * 128:(c + 1) * 128, :], (), ["wpB"])
                P.dma("pool", wout[:, c, :], w_o[c * 128:(c + 1) * 128, :], (), ["wout"])
            P.dma("pool", wup[0:17, :], w_up, (), ["wup"])
            P.dma("sp", ggl[:], g_gla, (), ["ggl"])
            P.dma("pool", Ui[:], c_ui, (), ["Ui"])
            P.memset("pool", Ones[:], 1.0, ["Ones"])
            P.dma("pool", Md[:], c_md, (), ["Md"])
            P.dma("pool", Tc[:], c_tc, (), ["Tc"])
            P.dma("pool", Ta[:], c_ta, (), ["Ta"])
            P.dma("sp", Mg[:], c_mg, (), ["Mg"])
            P.memset("dve", st32s[:], 0.0, ["st32"])
            P.memset("pool", stbf[:], 0.0, ["stbf"])
            P.memset("dve", lowA[:], 1.0, ["lowA"])

            f16v = sc_f16.rearrange("(c p) t -> p c t", p=128)
            f32v = sc_f32.rearrange("(c p) t -> p c t", p=128)

            def load_q(qb):
                t0 = qb * 128
                P.dma("sp", qTe[0:64, :, :], f16v[0:64, 0:4, t0:t0 + 128], [("sc_f16", t0 // TGA)], ["qT"])
                P.dma("sp", qTo[64:128, :, :], f16v[64:128, 0:4, t0:t0 + 128], [("sc_f16", t0 // TGA)], ["qT"])

            def load_rest(qb):
                t0 = qb * 128
                gA_ = t0 // TGA
                P.dma("sp", gq32[:], f32v[:, 0:4, t0:t0 + 128], [("sc_f32", gA_)], ["gq32"])
                P.dma("sp", gk32[:], f32v[:, 4:8, t0:t0 + 128], [("sc_f32", gA_)], ["gk32"])
                P.dma("sp", lowA[0:16, :], sc_low[:, t0:t0 + 128], [("sc_low", gA_)], ["lowA"])
                P.dma("sp", tgk[:], sc_t32[t0:t0 + 128, 0:512], [("sc_t32", qb)], ["tgk"])
                P.dma("sp", tgo[:], sc_t32[t0:t0 + 128, 512:1536], [("sc_t32", qb)], ["tgo"])
                P.dma("sp", tgA[:], sc_t32[t0:t0 + 128, 1536:2560], [("sc_t32", qb)], ["tgA"])
                P.dma("sp", tgB[:], sc_t32[t0:t0 + 128, 2560:3584], [("sc_t32", qb)], ["tgB"])
                P.dma("sp", gv16[:], sc_t16[t0:t0 + 128, 512:1536], [("sc_t16", qb)], ["gv16"])
                P.dma("sp", xres[:], x[t0:t0 + 128, :], (), ["xres"])

            P.memset("dve", qTe[:], 0.0, ["qT"])
            P.memset("dve", qTo[:], 0.0, ["qT"])
            load_q(0)
            load_rest(0)
            it = 0
            for qb in range(NT):
                t0 = qb * 128
                for kt in range(qb, -1, -1):
                    diag = (kt == qb)
                    for G in range(2):
                        bs = it % 2
                        it += 1
                        for hh in range(4):
                            h = 4 * G + hh
                            qz = qTe if h % 2 == 0 else qTo
                            P.mm(Z[G][:, hh * 128:(hh + 1) * 128], kT[:, h // 2, kt * 128:(kt + 1) * 128],
                                 qz[:, h // 2, :], True, True,
                                 [("kT", h // 2), "qT"], [("Z", G)])
                        P.act(e32[bs][:], Z[G][:], AF.Exp, [("Z", G)], [("e32", bs)])
                        P.act(spb[bs][:], e32[bs][:], AF.Ln, [("e32", bs)], [("spb", bs)], bias=1.0)
                        if diag:
                            P.tt("pool", spb[bs][:], spb[bs][:], Md[:], ALU.mult, [("spb", bs), "Md"], [("spb", bs)])
                        P.mm(T[G][:], Ui[:], spb[bs][:], True, diag, [("spb", bs), "Ui"], [("T", G)])
                        if not diag:
                            P.mm(T[G][:], Ones[:], Rb[G][:], False, True, [("Rb", G), "Ones"], [("T", G)])
                        P.act(g32[bs][:], T[G][:], AF.Exp, [("T", G)], [("g32", bs)], scale=-1.0)
                        if kt > 0:
                            if diag:
                                P.copy("pool", Rr[G][:], spb[bs][:], [("spb", bs)], [("Rr", G)])
                            else:
                                P.tt("pool", Rr[G][:], Rr[G][:], spb[bs][:], ALU.add, [("Rr", G), ("spb", bs)], [("Rr", G)])
                            P.copy("pool", Rb[G][:], Rr[G][:], [("Rr", G)], [("Rb", G)])
                        P.tt("dve", wbf[bs][:], e32[bs][:], g32[bs][:], ALU.mult, [("e32", bs), ("g32", bs)], [("wbf", bs)])
                        if diag:
                            P.tt("pool", wbf[bs][:], wbf[bs][:], Md[:], ALU.mult, [("wbf", bs), "Md"], [("wbf", bs)])
                        for hh in range(4):
                            h = 4 * G + hh
                            P.mm(O[0:64, h * 128:(h + 1) * 128], vS[:, kt, h * 64:(h + 1) * 64],
                                 wbf[bs][:, hh * 128:(hh + 1) * 128], diag and hh == 0, kt == 0 and hh == 3,
                                 [("vS", kt), ("wbf", bs)], ["O"])
                if qb + 1 < NT:
                    load_q(qb + 1)
                P.copy("act", oTb[:, 0:4, :].rearrange("p h t -> p (h t)"), O[0:64, 0:512], ["O"], ["oTb"])
                P.copy("dve", oTb[:, 4:8, :].rearrange("p h t -> p (h t)"), O[0:64, 512:1024], ["O"], ["oTb"])
                for hf in range(2):
                    for h in range(8):
                        P.mm(Z[hf][:], oTb[:, h, :], wpA[:, h, hf * 512:(hf + 1) * 512], h == 0, h == 7,
                             ["oTb", "wpA"], [("Z", hf)])
                P.act(tgA[:], tgA[:], AF.Exp, ["tgA"], ["tgA"], scale=-1.0)
                P.ts("dve", tgA[:], tgA[:], 1.0, None, ALU.add, None, ["tgA"], ["tgA"])
                P.recip("dve", tgA[:], tgA[:], ["tgA"], ["tgA"])
                for hf in range(2):
                    P.tt("dve", tgA[:, hf * 512:(hf + 1) * 512], Z[hf][:], tgA[:, hf * 512:(hf + 1) * 512], ALU.mult,
                         [("Z", hf), "tgA"], ["tgA"])

                P.mm(G0[:], lowA[0:17, :], wup[0:17, :], True, True, ["lowA", "wup"], ["G0"])
                P.act(eu[:], G0[:], AF.Exp, ["G0"], ["eu"], scale=-1.0)
                P.act(spg[:], eu[:], AF.Ln, ["eu"], ["spg"], bias=1.0)
                P.mm(G0[:], Ta[:], spg[:], True, True, ["Ta", "spg"], ["G0"])
                for h in range(4):
                    P.mm(Z[1][:, h * 128:(h + 1) * 128], spg[:, h * 128:(h + 1) * 128], Tc[:], True, True,
                         ["spg", "Tc"], [("Z", 1)])
                Ed = eu
                P.act(Ed[:], G0[:], AF.Exp, ["G0"], ["eu"])
                P.act(Eb[:], Z[1][:], AF.Exp, [("Z", 1)], ["Eb"])
                P.act(Enb[:], Z[1][:], AF.Exp, [("Z", 1)], ["Enb"], scale=-1.0)
                P.tt("dve", qtil[:], gq32[:].rearrange("p h t -> p (h t)"), Eb[:], ALU.mult, ["gq32", "Eb"], ["qtil"])
                P.tt("pool", ktil[:], gk32[:].rearrange("p h t -> p (h t)"), Enb[:], ALU.mult, ["gk32", "Enb"], ["ktil"])
                P.tt("pool", khat[:], tgk[:], Ed[:], ALU.mult, ["tgk", "eu"], ["khat"])
                for h in range(4):
                    P.mm(G0[:, h * 128:(h + 1) * 128], ktil[:, h * 128:(h + 1) * 128], qtil[:, h * 128:(h + 1) * 128],
                         True, True, ["ktil", "qtil"], ["G0"])
                P.tt("dve", scb[:], G0[:], Mg[:], ALU.mult, ["G0", "Mg"], ["scb"])
                for h in range(4):
                    ob = T[h // 2][:, (h % 2) * 256:(h % 2 + 1) * 256]
                    P.mm(ob, qtil[:, h * 128:(h + 1) * 128], stbf[:, h, :], True, False, ["qtil", "stbf"], [("T", h // 2)])
                    P.mm(ob, scb[:, h * 128:(h + 1) * 128], gv16[:, h * 256:(h + 1) * 256], False, True,
                         ["scb", "gv16"], [("T", h // 2)])
                for h in range(4):
                    P.mm(O[:, h * 256:(h + 1) * 256], khat[:, h * 128:(h + 1) * 128], gv16[:, h * 256:(h + 1) * 256],
                         True, True, ["khat", "gv16"], ["O"])
                for h in range(4):
                    P.stt("dve", st32s[:, h, :], st32s[:, h, :], Eb[:, h * 128 + 127:h * 128 + 128], O[:, h * 256:(h + 1) * 256],
                          ALU.mult, ALU.add, ["st32", "Eb", "O"], ["st32"])
                P.copy("pool", stbf[:], st32s[:], ["st32"], ["stbf"])
                for h in range(4):
                    ob = T[h // 2][:, (h % 2) * 256:(h % 2 + 1) * 256]
                    P.act(junk[h][:], ob, AF.Square, [("T", h // 2)], [("junk", h), ("ssg", h)], accum=ssg[:, h:h + 1])
                P.act(rsg[:], ssg[:], AF.Ln, [("ssg", h_) for h_ in range(4)], ["rsg"], bias=EPS, scale=1.0 / 256)
                P.act(rsg[:], rsg[:], AF.Exp, ["rsg"], ["rsg"], scale=-0.5)
                P.act(eg[:], tgo[:], AF.Exp, ["tgo"], ["eg"], scale=-1.0)
                P.ts("dve", eg[:], eg[:], 1.0, None, ALU.add, None, ["eg"], ["eg"])
                P.recip("dve", eg[:], eg[:], ["eg"], ["eg"])
                P.tt("pool", gs[:], tgo[:], eg[:], ALU.mult, ["tgo", "eg"], ["gs"])
                P.tt("pool", gs[:], gs[:], ggl[:], ALU.mult, ["gs", "ggl"], ["gs"])
                for h in range(4):
                    ob = T[h // 2][:, (h % 2) * 256:(h % 2 + 1) * 256]
                    P.stt("dve", ogb[:, h * 256:(h + 1) * 256], ob, rsg[:, h:h + 1], gs[:, h * 256:(h + 1) * 256],
                          ALU.mult, ALU.mult, [("T", h // 2), "rsg", "gs"], ["ogb"])
                G1b = G1[:]
                for c in range(8):
                    P.tr(G1b[:, c, :], ogb[:, c * 128:(c + 1) * 128], ident[:], ["ogb", "ident"], ["G1"])
                P.copy("act", ogT[:], G1b, ["G1"], ["ogT"])
                for hf in range(2):
                    for c in range(8):
                        P.mm(Z[hf][:], ogT[:, c, :], wpB[:, c, hf * 512:(hf + 1) * 512], c == 0, c == 7,
                             ["ogT", "wpB"], [("Z", hf)])
                P.act(tgB[:], tgB[:], AF.Exp, ["tgB"], ["tgB"], scale=-1.0)
                P.ts("dve", tgB[:], tgB[:], 1.0, None, ALU.add, None, ["tgB"], ["tgB"])
                P.recip("dve", tgB[:], tgB[:], ["tgB"], ["tgB"])
                for hf in range(2):
                    P.tt("dve", tgB[:, hf * 512:(hf + 1) * 512], Z[hf][:], tgB[:, hf * 512:(hf + 1) * 512], ALU.mult,
                         [("Z", hf), "tgB"], ["tgB"])
                mixb = ogb
                P.tt("pool", mixb[:], tgB[:], tgA[:], ALU.add, ["tgB", "tgA"], ["ogb"])
                for c in range(8):
                    P.tr(G1b[:, c, :], mixb[:, c * 128:(c + 1) * 128], ident[:], ["ogb", "ident"], ["G1"])
                mixT = ogT
                P.copy("act", mixT[:], G1b, ["G1"], ["ogT"])
                for hf in range(2):
                    for c in range(8):
                        P.mm(T[hf][:], mixT[:, c, :], wout[:, c, hf * 512:(hf + 1) * 512], c == 0, c == 7,
                             ["ogT", "wout"], [("T", hf)])
                for hf in range(2):
                    P.tt("dve", x1t[:, hf * 512:(hf + 1) * 512], T[hf][:], xres[:, hf * 512:(hf + 1) * 512], ALU.add,
                         [("T", hf), "xres"], ["x1t"])
                P.dma("sp", sc_x1[t0:t0 + 128, :], x1t[:], ["x1t"], [("sc_x1", qb)])
                if qb + 1 < NT:
                    load_rest(qb + 1)
            P.flush(barrier=True)
        if stop_after == "B":
            return nc

        with ExitStack() as es:
            def sb(name, shape, dt):
                return es.enter_context(nc.sbuf_tensor(name, list(shape), dt))

            def ps(name, shape, dt):
                return es.enter_context(nc.psum_tensor(name, list(shape), dt))

            W1 = sb("W1", [128, 8, DFF], BF16)
            W2 = sb("W2", [128, 32, D], BF16)
            gml = sb("gml", [128, D], F32)
            gfn = sb("gfn", [128, D], F32)
            xc = [sb("c_x%d" % i, [128, D], F32) for i in range(2 * TPC)]
            sqj = [sb("c_sqj%d" % i, [128, D], BF16) for i in range(2)]
            ss = [sb("c_ss%d" % i, [128, 1], F32) for i in range(2)]
            rs = [sb("c_rs%d" % i, [128, 1], F32) for i in range(2)]
            ss2 = [sb("c_ss2%d" % i, [128, 1], F32) for i in range(2)]
            rs2 = [sb("c_rs2%d" % i, [128, 1], F32) for i in range(2)]
            hb = [sb("c_hb%d" % i, [128, D], BF16) for i in range(2)]
            hT = [sb("c_hT%d" % i, [128, 8, TGC], BF16) for i in range(2)]
            aT = sb("c_aT", [128, 32, TGC], BF16)
            rl = [sb("c_rl%d" % i, [128, TGC], F32) for i in range(2)]
            x2 = [sb("c_x2%d" % i, [128, D], F32) for i in range(2)]
            ot = [sb("c_ot%d" % i, [128, D], F32) for i in range(2)]
            tp = [ps("c_tp%d" % i, [128, 8, 128], BF16) for i in range(2)]
            pu = [ps("c_pu%d" % i, [128, 512], F32) for i in range(3)]
            py = [ps("c_py%d" % i, [128, 512], F32) for i in range(3)]

            P.dma("sp", gml[:], g_mlp, (), ["gml"])
            P.dma("sp", gfn[:], g_fin, (), ["gfn"])
            for c in range(8):
                for q in range(4):
                    P.dma("pool", W1[:, c, q * 1024:(q + 1) * 1024], w_1[c * 128:(c + 1) * 128, q * 1024:(q + 1) * 1024],
                          (), [("W1", c)])
            for f in range(32):
                P.dma("pool", W2[:, f, :], w_2[f * 128:(f + 1) * 128, :], (), [("W2", f)])

            cntC = {"xi": 0, "ky": 0, "kx2": 0}
            xslots = {}

            def prepC(g):
                hs = g % 2
                xslots[g] = []
                for i in range(TPC):
                    t0 = g * TGC + i * 128
                    sl = cntC["xi"] % 2
                    xs = cntC["xi"] % (2 * TPC)
                    cntC["xi"] += 1
                    xslots[g].append(xs)
                    P.dma("sp", xc[xs][:], sc_x1[t0:t0 + 128, :], [("sc_x1", t0 // 128)], [("xc", xs)])
                    prenorm("c", xc[xs][:], ("xc", xs), gml, "gml", sqj, ss, rs, hb, sl, tp, sl,
                            hT[hs][:, :, i * 128:(i + 1) * 128], ("chT", hs))

            def computeC(g):
                hs = g % 2
                for f in range(32):
                    pb = pu[f % 3]
                    pk = ("pu", f % 3)
                    for c in range(8):
                        P.mm(pb[:, 0:TGC], W1[:, c, f * 128:(f + 1) * 128], hT[hs][:, c, :], c == 0, c == 7,
                             [("W1", c), ("chT", hs)], [pk])
                    r = rl[f % 2]
                    rk = ("rl", f % 2)
                    P.ts("dve", r[:], pb[:, 0:TGC], 0.0, None, ALU.max, None, [pk], [rk])
                    if f % 2 == 0:
                        P.act(aT[:, f, :], r[:], AF.Square, [rk], [("aT", f)])
                    else:
                        P.tt("pool", aT[:, f, :], r[:], r[:], ALU.mult, [rk], [("aT", f)])
                for i in range(TPC):
                    t0 = g * TGC + i * 128
                    xs = xslots[g][i]
                    xk = cntC["kx2"] % 2
                    cntC["kx2"] += 1
                    for hf in range(2):
                        pb = py[cntC["ky"] % 3]
                        pk = ("py", cntC["ky"] % 3)
                        cntC["ky"] += 1
                        for f in range(32):
                            P.mm(pb[:], aT[:, f, i * 128:(i + 1) * 128], W2[:, f, hf * 512:(hf + 1) * 512], f == 0, f == 31,
                                 [("aT", f), ("W2", f)], [pk])
                        P.tt("dve", x2[xk][:, hf * 512:(hf + 1) * 512], pb[:], xc[xs][:, hf * 512:(hf + 1) * 512], ALU.add,
                             [pk, ("xc", xs)], [("x2", xk)])
                    P.act(sqj[xk][:], x2[xk][:], AF.Square, [("x2", xk)], [("csqj", xk), ("fss", xk)], accum=ss2[xk][:])
                    P.act(rs2[xk][:], ss2[xk][:], AF.Ln, [("fss", xk)], [("frs", xk)], bias=EPS, scale=1.0 / D)
                    P.act(rs2[xk][:], rs2[xk][:], AF.Exp, [("frs", xk)], [("frs", xk)], scale=-0.5)
                    P.stt("dve", ot[xk][:], x2[xk][:], rs2[xk][:, 0:1], gfn[:], ALU.mult, ALU.mult,
                          [("x2", xk), ("frs", xk), "gfn"], [("ot", xk)])
                    P.dma("sp", out[t0:t0 + 128, :], ot[xk][:], [("ot", xk)], [("out", t0 // 128)])

            prepC(0)
            for g in range(NGC):
                if g + 1 < NGC:
                    prepC(g + 1)
                computeC(g)
            P.flush(barrier=True)
    return nc


def host_consts():
    j = np.arange(128)[:, None]
    s = np.arange(128)[None, :]
    f = np.float32
    c = {
        "c_id": (j == s).astype(f),
        "c_ui": (j >= s).astype(f),
        "c_ls": (j < s).astype(f),
        "c_md": np.tile((j < s).astype(f), (1, 4)),
        "c_tc": ((j <= s).astype(f) * f(-1.0 / 16)),
        "c_ta": ((j > s).astype(f) * f(-1.0 / 16)),
        "c_mg": np.tile((j <= s).astype(f), (1, 4)),
    }
    return {k: np.ascontiguousarray(v, dtype=f) for k, v in c.items()}


def make_in_map(xb, norm_mix, w_in, w_gate_up, b_gate_up, gla_norm, w_proj_sb, w_proj_gla,
                w_out, norm_mlp, w_ff1, w_ff2, norm_final, consts):
    f = np.float32
    rep = lambda v: np.ascontiguousarray(np.broadcast_to(np.asarray(v, f).reshape(1, -1), (128, v.size)))
    m = {
        "x": np.ascontiguousarray(xb, dtype=f),
        "w_in": np.ascontiguousarray(w_in[0], dtype=f),
        "w_up": np.ascontiguousarray(np.concatenate([w_gate_up[0], b_gate_up[0][None, :]], axis=0), dtype=f),
        "w_pa": np.ascontiguousarray(w_proj_sb[0], dtype=f),
        "w_pb": np.ascontiguousarray(w_proj_gla[0], dtype=f),
        "w_o": np.ascontiguousarray(w_out[0], dtype=f),
        "w_1": np.ascontiguousarray(w_ff1[0], dtype=f),
        "w_2": np.ascontiguousarray(w_ff2[0], dtype=f),
        "g_mix": rep(norm_mix[0]),
        "g_mlp": rep(norm_mlp[0]),
        "g_fin": rep(norm_final),
        "g_gla": rep(np.tile(np.asarray(gla_norm[0], f), 4)),
    }
    m.update(consts)
    return m


def kernel(x, norm_mix, w_in, w_gate_up, b_gate_up, gla_norm, w_proj_sb, w_proj_gla,
           w_out, norm_mlp, w_ff1, w_ff2, norm_final):
    x = np.asarray(x)
    B, S, _ = x.shape
    args = [np.asarray(a) for a in (norm_mix, w_in, w_gate_up, b_gate_up, gla_norm, w_proj_sb, w_proj_gla,
                                    w_out, norm_mlp, w_ff1, w_ff2, norm_final)]
    consts = host_consts()
    nc = build_program(S)
    in_maps = [make_in_map(x[b], *args, consts) for b in range(B)]
    res = run_bass_kernel_spmd(nc, in_maps, core_ids=list(range(B)))
    return np.stack([np.asarray(r["out"], dtype=np.float32) for r in res.results], axis=0)
```
